# Optimizing a Trainium2 kernel written in Bass

```python
import math
import jax, jax.numpy as jnp
from jax import lax
import numpy as np

D_MODEL = 2048
BATCH = 1
SEQ = 16384
DEPTH = 4

N_MIXERS = 2
N_MLA_LAYERS = (DEPTH + 1) // 2
N_SSM_LAYERS = DEPTH // 2
D_FF = 4 * D_MODEL
RMS_EPS = 1e-6

MLA_HEADS = 16
QK_NOPE_DIM = 128
QK_ROPE_DIM = 64
QK_HEAD_DIM = QK_NOPE_DIM + QK_ROPE_DIM
V_HEAD_DIM = 128
Q_LORA_RANK = 512
KV_LORA_RANK = 512
MLA_IN_DIM = Q_LORA_RANK + KV_LORA_RANK + QK_ROPE_DIM
ROPE_THETA = 10000.0
Q_BLOCK = 128

SSM_EXPAND = 2
SSM_D_INNER = SSM_EXPAND * D_MODEL
SSM_HEAD_DIM = 64
SSM_HEADS = SSM_D_INNER // SSM_HEAD_DIM
SSM_GROUPS = 8
SSM_HEADS_PER_GROUP = SSM_HEADS // SSM_GROUPS
SSM_STATE = 128
SSM_CONV_WIDTH = 4
SSM_CHUNK = 256
SSM_CONV_DIM = SSM_D_INNER + 2 * SSM_GROUPS * SSM_STATE
SSM_IN_DIM = SSM_D_INNER + SSM_CONV_DIM + SSM_HEADS

kernel_name = "hybrid_mla_mamba2_sqrelu_trunk"


def rms_norm(x, g):
    xf = x.astype(jnp.float32)
    y = xf * lax.rsqrt(jnp.mean(xf * xf, axis=-1, keepdims=True) + RMS_EPS)
    return (y * g.astype(jnp.float32)).astype(x.dtype)


def rope_tables(positions):
    inv_freq = ROPE_THETA ** (-jnp.arange(0, QK_ROPE_DIM, 2, dtype=jnp.float32) / QK_ROPE_DIM)
    ang = positions.astype(jnp.float32)[..., None] * inv_freq
    return jnp.cos(ang), jnp.sin(ang)


def apply_rope(x, cos, sin):
    x1, x2 = jnp.split(x, 2, axis=-1)
    c = cos[:, :, None, :].astype(x.dtype)
    s = sin[:, :, None, :].astype(x.dtype)
    return jnp.concatenate([x1 * c - x2 * s, x2 * c + x1 * s], axis=-1)


def causal_block_attention(q, k, v):
    B, S, H, Dq = q.shape
    n_blk = S // Q_BLOCK
    scale = Dq ** -0.5
    q_blocks = q.reshape(B, n_blk, Q_BLOCK, H, Dq).transpose(1, 0, 2, 3, 4)
    k_pos = jnp.arange(S)

    def one_block(args):
        blk_idx, q_blk = args
        s = jnp.einsum('bqhd,bkhd->bhqk', q_blk, k, preferred_element_type=jnp.float32) * scale
        q_pos = blk_idx * Q_BLOCK + jnp.arange(Q_BLOCK)
        mask = k_pos[None, :] <= q_pos[:, None]
        p = jax.nn.softmax(jnp.where(mask, s, -jnp.inf), axis=-1).astype(v.dtype)
        return jnp.einsum('bhqk,bkhd->bqhd', p, v)

    o = lax.map(one_block, (jnp.arange(n_blk), q_blocks))
    return o.transpose(1, 0, 2, 3, 4).reshape(B, S, H, v.shape[-1])


def mla_mixer(h, cos, sin, w_in, q_norm_g, w_uq, kv_norm_g, w_ukv, qn_g, kn_g, w_o):
    B, S, _ = h.shape
    a = h @ w_in
    c_q, c_kv, k_rope = jnp.split(a, [Q_LORA_RANK, Q_LORA_RANK + KV_LORA_RANK], axis=-1)
    q = (rms_norm(c_q, q_norm_g) @ w_uq).reshape(B, S, MLA_HEADS, QK_HEAD_DIM)
    kv = (rms_norm(c_kv, kv_norm_g) @ w_ukv).reshape(B, S, MLA_HEADS, QK_NOPE_DIM + V_HEAD_DIM)
    k_nope, v = jnp.split(kv, [QK_NOPE_DIM], axis=-1)
    k_rope = jnp.broadcast_to(k_rope[:, :, None, :], (B, S, MLA_HEADS, QK_ROPE_DIM))
    k = jnp.concatenate([k_nope, k_rope], axis=-1)
    q = rms_norm(q, qn_g)
    k = rms_norm(k, kn_g)
    q = jnp.concatenate([q[..., :QK_NOPE_DIM], apply_rope(q[..., QK_NOPE_DIM:], cos, sin)], axis=-1)
    k = jnp.concatenate([k[..., :QK_NOPE_DIM], apply_rope(k[..., QK_NOPE_DIM:], cos, sin)], axis=-1)
    o = causal_block_attention(q, k, v)
    return o.reshape(B, S, MLA_HEADS * V_HEAD_DIM) @ w_o


def causal_depthwise_conv(x, w, b):
    C = x.shape[-1]
    out = lax.conv_general_dilated(
        x, w[:, None, :], window_strides=(1,), padding=[(SSM_CONV_WIDTH - 1, 0)],
        dimension_numbers=('NWC', 'WIO', 'NWC'), feature_group_count=C)
    return out + b


def ssd_chunked_scan(x, dt, a, b, c):
    B, S, G, HG, P = x.shape
    N = b.shape[-1]
    L = math.gcd(S, SSM_CHUNK)
    nc = S // L
    xdt = x.astype(jnp.float32) * dt[..., None]
    log_a = dt * a

    def to_chunks(t):
        return t.reshape((B, nc, L) + t.shape[2:]).swapaxes(0, 1)

    xs = (to_chunks(xdt), to_chunks(log_a),
          to_chunks(b.astype(jnp.float32)), to_chunks(c.astype(jnp.float32)))
    tril = jnp.tril(jnp.ones((L, L), dtype=bool))

    def chunk_step(state, inp):
        x_c, la_c, b_c, c_c = inp
        cum = jnp.cumsum(la_c, axis=1)
        diff = cum[:, :, None] - cum[:, None, :]
        decay = jnp.exp(jnp.where(tril[None, :, :, None, None], diff, -jnp.inf))
        cb = jnp.einsum('blgn,bsgn->blsg', c_c, b_c)
        y_diag = jnp.einsum('blsgh,bsghp->blghp', cb[..., None] * decay, x_c)
        y_off = jnp.einsum('blgn,bghpn->blghp', c_c, state) * jnp.exp(cum)[..., None]
        decay_to_end = jnp.exp(cum[:, -1:] - cum)
        new_state = state * jnp.exp(cum[:, -1])[..., None, None] + jnp.einsum(
            'blgn,blghp->bghpn', b_c, x_c * decay_to_end[..., None])
        return new_state, y_diag + y_off

    state0 = jnp.zeros((B, G, HG, P, N), jnp.float32)
    _, ys = lax.scan(chunk_step, state0, xs)
    return ys.swapaxes(0, 1).reshape(B, S, G, HG, P)


def ssm_mixer(h, w_in, conv_w, conv_b, dt_bias, a_log, d_skip, norm_g, w_out):
    B, S, _ = h.shape
    zxbcdt = h @ w_in
    z, xbc, dt = jnp.split(zxbcdt, [SSM_D_INNER, SSM_D_INNER + SSM_CONV_DIM], axis=-1)
    xbc = jax.nn.silu(causal_depthwise_conv(xbc, conv_w, conv_b))
    xs, b_in, c_in = jnp.split(xbc, [SSM_D_INNER, SSM_D_INNER + SSM_GROUPS * SSM_STATE], axis=-1)
    xs = xs.reshape(B, S, SSM_GROUPS, SSM_HEADS_PER_GROUP, SSM_HEAD_DIM)
    b_in = b_in.reshape(B, S, SSM_GROUPS, SSM_STATE)
    c_in = c_in.reshape(B, S, SSM_GROUPS, SSM_STATE)
    dt = jax.nn.softplus(dt.astype(jnp.float32) + dt_bias.astype(jnp.float32))
    dt = dt.reshape(B, S, SSM_GROUPS, SSM_HEADS_PER_GROUP)
    a = -jnp.exp(a_log.astype(jnp.float32)).reshape(SSM_GROUPS, SSM_HEADS_PER_GROUP)
    y = ssd_chunked_scan(xs, dt, a, b_in, c_in)
    y = y + d_skip.astype(jnp.float32).reshape(SSM_GROUPS, SSM_HEADS_PER_GROUP)[..., None] * xs.astype(jnp.float32)
    y = y.reshape(B, S, SSM_D_INNER) * jax.nn.silu(z.astype(jnp.float32))
    y = y.reshape(B, S, SSM_GROUPS, SSM_D_INNER // SSM_GROUPS)
    y = y * lax.rsqrt(jnp.mean(y * y, axis=-1, keepdims=True) + RMS_EPS)
    y = (y.reshape(B, S, SSM_D_INNER) * norm_g.astype(jnp.float32)).astype(h.dtype)
    return y @ w_out


def squared_relu_mlp(h, w_in, w_out):
    u = jax.nn.relu(h @ w_in)
    return (u * u) @ w_out


def setup_inputs(seed: int = 0) -> dict:
    key = jax.random.key(seed)
    ks = jax.random.split(key, 24)
    f32 = jnp.float32
    res_scale = (2 * DEPTH) ** -0.5

    def w(k, shape, fan_in, scale=1.0):
        return jax.random.normal(k, shape, f32) * (scale * fan_in ** -0.5)

    def gain(k, shape):
        return 1.0 + 0.02 * jax.random.normal(k, shape, f32)

    NA, NB = N_MLA_LAYERS, N_SSM_LAYERS
    x = jax.random.normal(ks[0], (BATCH, SEQ, D_MODEL), f32)
    offset = jax.random.randint(ks[1], (BATCH, 1), 0, 4096, dtype=jnp.int32)
    positions = offset + jnp.arange(SEQ, dtype=jnp.int32)[None, :]
    dt0 = jnp.exp(jax.random.uniform(ks[17], (NB, SSM_HEADS), f32,
                                     minval=math.log(1e-3), maxval=math.log(1e-1)))
    return {
        "x": x,
        "positions": positions,
        "mix_norm_g": gain(ks[2], (DEPTH, D_MODEL)),
        "mlp_norm_g": gain(ks[3], (DEPTH, D_MODEL)),
        "mlp_w_in": w(ks[4], (DEPTH, D_MODEL, D_FF), D_MODEL),
        "mlp_w_out": w(ks[5], (DEPTH, D_FF, D_MODEL), D_FF, res_scale),
        "mla_w_in": w(ks[6], (NA, D_MODEL, MLA_IN_DIM), D_MODEL),
        "mla_q_norm_g": gain(ks[7], (NA, Q_LORA_RANK)),
        "mla_w_uq": w(ks[8], (NA, Q_LORA_RANK, MLA_HEADS * QK_HEAD_DIM), Q_LORA_RANK),
        "mla_kv_norm_g": gain(ks[9], (NA, KV_LORA_RANK)),
        "mla_w_ukv": w(ks[10], (NA, KV_LORA_RANK, MLA_HEADS * (QK_NOPE_DIM + V_HEAD_DIM)), KV_LORA_RANK),
        "mla_qk_norm_q": gain(ks[11], (NA, QK_HEAD_DIM)),
        "mla_qk_norm_k": gain(ks[12], (NA, QK_HEAD_DIM)),
        "mla_w_o": w(ks[13], (NA, MLA_HEADS * V_HEAD_DIM, D_MODEL), MLA_HEADS * V_HEAD_DIM, res_scale),
        "ssm_w_in": w(ks[14], (NB, D_MODEL, SSM_IN_DIM), D_MODEL),
        "ssm_conv_w": w(ks[15], (NB, SSM_CONV_WIDTH, SSM_CONV_DIM), SSM_CONV_WIDTH),
        "ssm_conv_b": 0.02 * jax.random.normal(ks[16], (NB, SSM_CONV_DIM), f32),
        "ssm_dt_bias": dt0 + jnp.log(-jnp.expm1(-dt0)),
        "ssm_a_log": jnp.log(jax.random.uniform(ks[18], (NB, SSM_HEADS), f32, minval=1.0, maxval=16.0)),
        "ssm_d": gain(ks[19], (NB, SSM_HEADS)),
        "ssm_norm_g": gain(ks[20], (NB, SSM_D_INNER)),
        "ssm_w_out": w(ks[21], (NB, SSM_D_INNER, D_MODEL), SSM_D_INNER, res_scale),
    }


def reference(x, positions, mix_norm_g, mlp_norm_g, mlp_w_in, mlp_w_out,
              mla_w_in, mla_q_norm_g, mla_w_uq, mla_kv_norm_g, mla_w_ukv,
              mla_qk_norm_q, mla_qk_norm_k, mla_w_o,
              ssm_w_in, ssm_conv_w, ssm_conv_b, ssm_dt_bias, ssm_a_log, ssm_d,
              ssm_norm_g, ssm_w_out):
    cos, sin = rope_tables(positions)
    for i in range(DEPTH):
        h = rms_norm(x, mix_norm_g[i])
        j = i // N_MIXERS
        if i % N_MIXERS == 0:
            mix = mla_mixer(h, cos, sin, mla_w_in[j], mla_q_norm_g[j], mla_w_uq[j],
                            mla_kv_norm_g[j], mla_w_ukv[j], mla_qk_norm_q[j],
                            mla_qk_norm_k[j], mla_w_o[j])
        else:
            mix = ssm_mixer(h, ssm_w_in[j], ssm_conv_w[j], ssm_conv_b[j], ssm_dt_bias[j],
                            ssm_a_log[j], ssm_d[j], ssm_norm_g[j], ssm_w_out[j])
        x = x + mix
        h = rms_norm(x, mlp_norm_g[i])
        x = x + squared_relu_mlp(h, mlp_w_in[i], mlp_w_out[i])
    return x
```

```python
import contextlib
import ml_dtypes
from concourse.bass_utils import run_bass_kernel_spmd
import numpy as np
import concourse.bass as bass
import concourse.mybir as mybir

DT = mybir.dt
F32 = DT.float32
BF16 = DT.bfloat16
AF = mybir.ActivationFunctionType
ALU = mybir.AluOpType
AX = mybir.AxisListType

SEM_LIMIT = 30000


class T:
    __slots__ = ("name", "writers", "readers")

    def __init__(self, name):
        self.name = name
        self.writers = []
        self.readers = []


class Op:
    __slots__ = ("eng", "fn", "deps", "is_dma", "ndma", "key", "sig", "idx",
                 "sem_i", "count", "waits")


class Prog:
    ENGS = ("pe", "act", "dve", "pool", "sp")

    def __init__(self, nc, same_engine_sync=True):
        self.nc = nc
        self.ops = []
        self.by_eng = {e: [] for e in self.ENGS}
        self.dma_keys = {}
        self.same_engine_sync = same_engine_sync
        self.stack = contextlib.ExitStack()
        self._n = 0

    def sbuf(self, name, shape, dtype):
        return self.stack.enter_context(self.nc.sbuf_tensor(name, list(shape), dtype))

    def psum(self, name, shape, dtype=F32):
        return self.stack.enter_context(self.nc.psum_tensor(name, list(shape), dtype))

    def T(self, name=None):
        self._n += 1
        return T(name or f"t{self._n}")

    def op(self, eng, fn, reads=(), writes=(), dma_key=None, ndma=1):
        o = Op()
        o.eng = eng
        o.fn = fn
        o.is_dma = dma_key is not None
        o.key = dma_key
        o.ndma = ndma
        o.sig = False
        o.idx = len(self.ops)
        deps = []
        for t in reads:
            deps.extend(t.writers)
        for t in writes:
            deps.extend(t.readers)
            deps.extend(t.writers)
        if o.is_dma:
            k = self.dma_keys.setdefault(dma_key, dict(cum=0, last=None, ops=[]))
            if k["last"] is not None:
                deps.append(k["last"])
        red = {}
        for d in deps:
            if d is o:
                continue
            kk = ("dma", d.key) if d.is_dma else ("eng", d.eng)
            if kk not in red or red[kk].idx < d.idx:
                red[kk] = d
        final = []
        for kk, d in red.items():
            if kk[0] == "eng":
                if d.eng == eng and not o.is_dma:
                    if eng == "pe" or not self.same_engine_sync:
                        continue
                elif d.eng == eng and o.is_dma:
                    pass
            final.append(d)
            d.sig = True
        o.deps = final
        for t in reads:
            t.readers.append(o)
        for t in writes:
            if t.readers:
                t.writers = [o]
                t.readers = []
            else:
                t.writers.append(o)
        if o.is_dma:
            k["last"] = o
            k["ops"].append(o)
        self.ops.append(o)
        self.by_eng[eng].append(o)
        return o

    def pe(self, fn, reads=(), writes=()):
        return self.op("pe", fn, reads, writes)

    def act(self, fn, reads=(), writes=()):
        return self.op("act", fn, reads, writes)

    def dve(self, fn, reads=(), writes=()):
        return self.op("dve", fn, reads, writes)

    def pool(self, fn, reads=(), writes=()):
        return self.op("pool", fn, reads, writes)

    def dma(self, fn, reads=(), writes=(), key=None, ndma=1, eng="sp"):
        assert key is not None
        return self.op(eng, fn, reads, writes, dma_key=key, ndma=ndma)

    def emit(self):
        nc = self.nc
        self.sems = {}

        def get_sem(kind, name, i):
            kk = (kind, name, i)
            if kk not in self.sems:
                self.sems[kk] = nc.alloc_semaphore(name=f"s_{kind}_{name}_{i}".replace(" ", "_"))
            return self.sems[kk]

        for e in self.ENGS:
            c = 0
            for o in self.by_eng[e]:
                if o.is_dma:
                    continue
                if o.sig:
                    c += 1
                    o.sem_i = (c - 1) // SEM_LIMIT
                    o.count = (c - 1) % SEM_LIMIT + 1
        for key, k in self.dma_keys.items():
            c = 0
            ep = 0
            for o in k["ops"]:
                if c + 16 * o.ndma > SEM_LIMIT:
                    ep += 1
                    c = 0
                c += 16 * o.ndma
                o.sem_i = ep
                o.count = c
        final_waits = []
        for key, k in self.dma_keys.items():
            final_waits.append(k["ops"][-1])

        def emit_eng(ename, eng):
            known = {}
            for o in self.by_eng[ename]:
                for d in o.deps:
                    if d.is_dma:
                        sk = ("dma", str(d.key), d.sem_i)
                    else:
                        sk = ("eng", d.eng, d.sem_i)
                    if known.get(sk, 0) >= d.count:
                        continue
                    known[sk] = d.count
                    eng.wait_ge(get_sem(*sk), d.count)
                r = o.fn(eng)
                if o.is_dma:
                    insts = r if isinstance(r, (list, tuple)) else [r]
                    assert len(insts) == o.ndma, (len(insts), o.ndma, o.key)
                    s = get_sem("dma", str(o.key), o.sem_i)
                    for ins in insts:
                        ins.then_inc(s, 16)
                elif o.sig:
                    r.then_inc(get_sem("eng", ename, o.sem_i), 1)
            if ename == "sp":
                for d in final_waits:
                    sk = ("dma", str(d.key), d.sem_i)
                    if known.get(sk, 0) >= d.count:
                        continue
                    eng.wait_ge(get_sem(*sk), d.count)

        with nc.Block() as block:
            @block.tensor
            def _(eng):
                emit_eng("pe", eng)

            @block.scalar
            def _(eng):
                emit_eng("act", eng)

            @block.vector
            def _(eng):
                emit_eng("dve", eng)

            @block.gpsimd
            def _(eng):
                emit_eng("pool", eng)

            @block.sync
            def _(eng):
                emit_eng("sp", eng)
        self.stack.close()

D = 2048
DFF = 8192
NCORES = 8
RMS_EPS = 1e-6
BF = ml_dtypes.bfloat16


def _dr(nc, name, shape, dtype, kind):
    return nc.dram_tensor(name, list(shape), dtype, kind=kind).ap()


class NormT:
    def __init__(self, P, gbc, t_g, ident, t_id):
        self.P = P
        self.gbc, self.t_g, self.ident, self.t_id = gbc, t_g, ident, t_id
        self.hb = [P.sbuf(f"nt_hb{i}", [128, D], BF16) for i in range(2)]
        self.t_hb = [P.T(), P.T()]
        self.junk = P.sbuf("nt_junk", [128, D], BF16)
        self.t_junk = P.T()
        self.ss = P.sbuf("nt_ss", [128, 8], F32)
        self.t_ss = [P.T() for _ in range(4)]
        self.pT = [P.psum(f"nt_pT{i}", [128, 1024], BF16) for i in range(2)]
        self.t_pT = [P.T(), P.T()]
        self.cnt = 0

    def emit(self, xt, t_x, hT, t_h):
        P = self.P
        for blk in range(4):
            i = self.cnt % 2
            self.cnt += 1
            hb, t_hb = self.hb[i], self.t_hb[i]
            ss, t_ss = self.ss, self.t_ss[blk]
            junk = self.junk
            P.act(lambda e, blk=blk: e.activation(out=junk[:], in_=xt[:, blk, :], func=AF.Square,
                                                  accum_out=ss[:, blk:blk + 1]),
                  reads=[t_x[blk]], writes=[self.t_junk, t_ss])
            P.act(lambda e, blk=blk: e.activation(out=ss[:, 4 + blk:5 + blk], in_=ss[:, blk:blk + 1],
                                                  func=AF.Sqrt, scale=1.0 / D, bias=RMS_EPS),
                  reads=[t_ss], writes=[t_ss])
            P.dve(lambda e, blk=blk: e.reciprocal(out=ss[:, 4 + blk:5 + blk], in_=ss[:, 4 + blk:5 + blk]),
                  reads=[t_ss], writes=[t_ss])
            P.dve(lambda e, blk=blk, hb=hb: e.scalar_tensor_tensor(
                out=hb[:], in0=xt[:, blk, :], scalar=ss[:, 4 + blk:5 + blk], in1=self.gbc[:],
                op0=ALU.mult, op1=ALU.mult),
                reads=[t_x[blk], t_ss, self.t_g], writes=[t_hb])
            for half in range(2):
                pT, t_pT = self.pT[half], self.t_pT[half]
                for kk in range(8):
                    k = half * 8 + kk
                    P.pe(lambda e, kk=kk, k=k, pT=pT, hb=hb: e.transpose(
                        pT[:, kk * 128:(kk + 1) * 128], hb[:, k * 128:(k + 1) * 128], self.ident[:]),
                        reads=[t_hb, self.t_id], writes=[t_pT])
                dst = hT[:, half * 8:(half + 1) * 8, blk * 128:(blk + 1) * 128]
                src = pT[:].rearrange("p (k c) -> p k c", k=8)
                if half == 0:
                    P.act(lambda e, dst=dst, src=src: e.activation(out=dst, in_=src, func=AF.Copy),
                          reads=[t_pT], writes=[t_h])
                else:
                    P.dve(lambda e, dst=dst, src=src: e.tensor_copy(out=dst, in_=src),
                          reads=[t_pT], writes=[t_h])


class AccGemm:
    def __init__(self, P, nbuf=3):
        self.P = P
        self.wb = [P.sbuf(f"wb{i}", [128, 16, 512], BF16) for i in range(nbuf)]
        self.t_w = [P.T() for _ in range(nbuf)]
        self.wi = 0
        self.pa = [P.psum(f"pa{i}", [128, 512], F32) for i in range(4)]
        self.t_pa = [P.T() for _ in range(4)]

    def load_w(self, w_ap):
        i = self.wi % len(self.wb)
        self.wi += 1
        wb, t_w = self.wb[i], self.t_w[i]
        src = w_ap.rearrange("(k p) n -> p k n", p=128)
        self.P.dma(lambda e: e.dma_start(out=wb[:], in_=src), writes=[t_w], key=f"w{i}", eng="pool")
        return wb, t_w

    def emit(self, XT, t_XT, Wd, row0, xt, t_x):
        P = self.P
        for n in range(4):
            wb, t_w = self.load_w(Wd[row0:row0 + 2048, n * 512:(n + 1) * 512])
            for blk in range(4):
                pa, t_pa = self.pa[blk], self.t_pa[blk]
                for k in range(16):
                    P.pe(lambda e, k=k, blk=blk, pa=pa, wb=wb: e.matmul(
                        pa[:], lhsT=XT[:, k, blk * 128:(blk + 1) * 128], rhs=wb[:, k, :],
                        start=(k == 0), stop=(k == 15)),
                        reads=list(t_XT) + [t_w], writes=[t_pa])
                xs = xt[:, blk, n * 512:(n + 1) * 512]
                P.dve(lambda e, pa=pa, xs=xs: e.tensor_tensor(out=xs, in0=pa[:], in1=xs, op=ALU.add),
                      reads=[t_pa, t_x[blk]], writes=[t_x[blk]])


def build_post_mlp(KC, TPC):
    nc = bass.Bass("TRN2", target_bir_lowering=False)
    NT = TPC // 512
    x_in = _dr(nc, "x", [TPC, D], F32, "ExternalInput")
    yT_d = _dr(nc, "yT", [KC * 128, TPC], BF16, "ExternalInput")
    wp_d = _dr(nc, "wp", [KC * 128, D], F32, "ExternalInput")
    g_d = _dr(nc, "g_bc", [128, D], F32, "ExternalInput")
    w1_d = _dr(nc, "w1", [D, DFF], F32, "ExternalInput")
    w2_d = _dr(nc, "w2", [DFF, D], F32, "ExternalInput")
    id_d = _dr(nc, "ident", [128, 128], BF16, "ExternalInput")
    x_out = _dr(nc, "x_out", [TPC, D], F32, "ExternalOutput")
    P = Prog(nc)
    xt = P.sbuf("xt", [128, 4, D], F32)
    t_x = [P.T() for _ in range(4)]
    yt = P.sbuf("yt", [128, 16, 512], BF16)
    t_y = P.T()
    hT = P.sbuf("hT", [128, 16, 512], BF16)
    t_h = P.T()
    uT = [P.sbuf(f"uT{i}", [128, 16, 512], BF16) for i in range(2)]
    t_u = [[P.T() for _ in range(4)] for _ in range(2)]
    gbc = P.sbuf("gbc", [128, D], F32)
    t_g = P.T()
    ident = P.sbuf("ident_sb", [128, 128], BF16)
    t_id = P.T()
    sq = [P.sbuf(f"sq{i}", [128, 512], F32) for i in range(2)]
    t_sq = [P.T(), P.T()]
    pu = [P.psum(f"pu{i}", [128, 512], F32) for i in range(2)]
    t_pu = [P.T(), P.T()]
    P.dma(lambda e: e.dma_start(out=gbc[:], in_=g_d), writes=[t_g], key="c_g")
    P.dma(lambda e: e.dma_start(out=ident[:], in_=id_d), writes=[t_id], key="c_id")
    nt = NormT(P, gbc, t_g, ident, t_id)
    ag = AccGemm(P)
    ucnt = 0
    for t in range(NT):
        tok0 = t * 512
        for blk in range(4):
            P.dma(lambda e, blk=blk, tok0=tok0: e.dma_start(
                out=xt[:, blk, :], in_=x_in[tok0 + blk * 128: tok0 + (blk + 1) * 128, :]),
                writes=[t_x[blk]], key=f"x{blk}")
        for piece in range(KC // 16):
            src = yT_d[piece * 2048:(piece + 1) * 2048, tok0:tok0 + 512].rearrange("(k p) t -> p k t", p=128)
            P.dma(lambda e, src=src: e.dma_start(out=yt[:], in_=src), writes=[t_y], key="yt")
            ag.emit(yt, [t_y], wp_d, piece * 2048, xt, t_x)
        nt.emit(xt, t_x, hT, t_h)
        for q in range(4):
            u, tu = uT[q % 2], t_u[q % 2]
            for fg in range(4):
                c0 = (q * 4 + fg) * 512
                wb, t_w = ag.load_w(w1_d[:, c0:c0 + 512])
                for fl in range(4):
                    f = fg * 4 + fl
                    i = ucnt % 2
                    ucnt += 1
                    for k in range(16):
                        P.pe(lambda e, k=k, fl=fl, i=i, wb=wb: e.matmul(
                            pu[i][:], lhsT=wb[:, k, fl * 128:(fl + 1) * 128], rhs=hT[:, k, :],
                            start=(k == 0), stop=(k == 15)),
                            reads=[t_w, t_h], writes=[t_pu[i]])
                    P.act(lambda e, i=i: e.activation(out=sq[i][:], in_=pu[i][:], func=AF.Square),
                          reads=[t_pu[i]], writes=[t_sq[i]])
                    P.dve(lambda e, i=i, u=u, f=f: e.scalar_tensor_tensor(
                        out=u[:, f, :], in0=pu[i][:], scalar=0.0, in1=sq[i][:],
                        op0=ALU.is_gt, op1=ALU.mult),
                        reads=[t_pu[i], t_sq[i]], writes=[tu[fg]])
            ag.emit(u, tu, w2_d, q * 2048, xt, t_x)
        for blk in range(4):
            P.dma(lambda e, blk=blk, tok0=tok0: e.dma_start(
                out=x_out[tok0 + blk * 128: tok0 + (blk + 1) * 128, :], in_=xt[:, blk, :]),
                reads=[t_x[blk]], key=f"xo{blk}")
    P.emit()
    return nc


def build_norm_T(TPC):
    nc = bass.Bass("TRN2", target_bir_lowering=False)
    NT = TPC // 512
    x_in = _dr(nc, "x", [TPC, D], F32, "ExternalInput")
    g_d = _dr(nc, "g_bc", [128, D], F32, "ExternalInput")
    id_d = _dr(nc, "ident", [128, 128], BF16, "ExternalInput")
    hT_d = _dr(nc, "hT", [D, TPC], BF16, "ExternalOutput")
    P = Prog(nc)
    xt = [P.sbuf(f"xt{i}", [128, 4, D], F32) for i in range(2)]
    t_x = [[P.T() for _ in range(4)] for _ in range(2)]
    hT = [P.sbuf(f"hT{i}", [128, 16, 512], BF16) for i in range(2)]
    t_h = [P.T(), P.T()]
    gbc = P.sbuf("gbc", [128, D], F32)
    t_g = P.T()
    ident = P.sbuf("ident_sb", [128, 128], BF16)
    t_id = P.T()
    P.dma(lambda e: e.dma_start(out=gbc[:], in_=g_d), writes=[t_g], key="c_g")
    P.dma(lambda e: e.dma_start(out=ident[:], in_=id_d), writes=[t_id], key="c_id")
    nt = NormT(P, gbc, t_g, ident, t_id)
    for t in range(NT):
        tok0 = t * 512
        b = t % 2
        for blk in range(4):
            P.dma(lambda e, blk=blk, tok0=tok0, b=b: e.dma_start(
                out=xt[b][:, blk, :], in_=x_in[tok0 + blk * 128: tok0 + (blk + 1) * 128, :]),
                writes=[t_x[b][blk]], key=f"x{b}{blk}")
        nt.emit(xt[b], t_x[b], hT[b], t_h[b])
        dst = hT_d[:, tok0:tok0 + 512].rearrange("(k p) t -> p k t", p=128)
        P.dma(lambda e, dst=dst, b=b: e.dma_start(out=dst, in_=hT[b][:]), reads=[t_h[b]], key=f"ho{b}")
    P.emit()
    return nc


def _bc(v, n=128):
    v = np.asarray(v, np.float32).reshape(1, -1)
    return np.ascontiguousarray(np.broadcast_to(v, (n, v.shape[1])))


def _run(nc, in_maps):
    res = run_bass_kernel_spmd(nc, in_maps, core_ids=list(range(NCORES)))
    return res.results


def _consts():
    ident = np.eye(128, dtype=np.float32)
    return dict(
        ident=ident.astype(BF), identf=ident, ones=np.ones((128, 128), BF),
        tri=np.triu(np.ones((128, 128), np.float32)).astype(BF),
        trif=np.triu(np.ones((128, 128), np.float32)),
    )


def kernel(x, positions, mix_norm_g, mlp_norm_g, mlp_w_in, mlp_w_out,
           mla_w_in, mla_q_norm_g, mla_w_uq, mla_kv_norm_g, mla_w_ukv,
           mla_qk_norm_q, mla_qk_norm_k, mla_w_o,
           ssm_w_in, ssm_conv_w, ssm_conv_b, ssm_dt_bias, ssm_a_log, ssm_d,
           ssm_norm_g, ssm_w_out):
    f32 = lambda a: np.ascontiguousarray(np.asarray(a, dtype=np.float32))
    x = f32(x)
    S = x.shape[1]
    TPC = S // NCORES
    depth = mix_norm_g.shape[0]
    C = _consts()
    xs = [np.ascontiguousarray(x[0, c * TPC:(c + 1) * TPC]) for c in range(NCORES)]
    pos64 = np.ascontiguousarray(np.broadcast_to(np.asarray(positions, np.int32).reshape(1, S), (64, S)))
    invf = rope_consts()
    swap = np.concatenate([np.arange(160, 192), np.arange(128, 160)])
    sel = np.zeros((8, 8, 128), np.float32)
    for h in range(8):
        sel[h, h, :] = 1.0
    sel = sel.reshape(8, 1024)

    def gpack(g):
        g = np.asarray(g, np.float32)
        o = np.zeros((128, 3), np.float32)
        o[:, 0] = g[:128]
        o[:64, 1] = g[128:192]
        o[:64, 2] = g[swap]
        return o

    for i in range(depth):
        j = i // 2
        if i % 2 == 0:
            g_lat = np.ascontiguousarray(np.concatenate(
                [f32(mla_q_norm_g[j]).reshape(4, 128).T, f32(mla_kv_norm_g[j]).reshape(4, 128).T], 1))
            common = dict(g_bc=_bc(mix_norm_g[i]), ident=C["ident"], ones=C["ones"],
                          w_in=f32(mla_w_in[j]), g_lat=g_lat)
            r = _run(build_mla_pre(TPC), [dict(common, x=xs[c]) for c in range(NCORES)])
            AT = np.ascontiguousarray(np.concatenate([r[c]["aT"] for c in range(NCORES)], axis=1))
            wuq = f32(mla_w_uq[j]).reshape(QLR, 16, QK)
            wukv = f32(mla_w_ukv[j]).reshape(KVLR, 16, NOPE + DV)
            common = dict(AT=AT, pos64=pos64, invf=invf, ones=C["ones"], tri=C["tri"],
                          gq=gpack(mla_qk_norm_q[j]), gk=gpack(mla_qk_norm_k[j]))
            maps = []
            for c in range(NCORES):
                wq = np.stack([np.concatenate([wuq[:, h, 0:128], wuq[:, h, 128:192], wuq[:, h, swap]], 1)
                               for h in (2 * c, 2 * c + 1)])
                wkv = np.stack([wukv[:, h, :] for h in (2 * c, 2 * c + 1)])
                maps.append(dict(common, wq=np.ascontiguousarray(wq), wkv=np.ascontiguousarray(wkv)))
            r = _run(build_attn(S), maps)
            YT = np.concatenate([r[c]["oT"] for c in range(NCORES)], axis=0)
            KC = 16
            wp = f32(mla_w_o[j])
        else:
            common = dict(g_bc=_bc(mix_norm_g[i]), ident=C["ident"])
            r = _run(build_norm_T(TPC), [dict(common, x=xs[c]) for c in range(NCORES)])
            HT = np.ascontiguousarray(np.concatenate([r[c]["hT"] for c in range(NCORES)], axis=1))
            w = f32(ssm_w_in[j])
            cwf = f32(ssm_conv_w[j])
            cbf = f32(ssm_conv_b[j])
            maps = []
            for g in range(NCORES):
                chans = [slice(512 * g + 128 * q, 512 * g + 128 * (q + 1)) for q in range(4)]
                chans += [slice(4096 + 128 * g, 4096 + 128 * (g + 1)), slice(5120 + 128 * g, 5120 + 128 * (g + 1))]
                cw = np.zeros((128, 24), np.float32)
                cb = np.zeros((128, 6), np.float32)
                for q, ch in enumerate(chans):
                    cw[:, q * 4:(q + 1) * 4] = cwf[:, ch].T
                    cb[:, q] = cbf[ch]
                hp = np.stack([f32(ssm_dt_bias[j])[8 * g:8 * g + 8], f32(ssm_a_log[j])[8 * g:8 * g + 8]], 1)
                maps.append(dict(
                    HT=HT,
                    wz=np.ascontiguousarray(w[:, 512 * g:512 * (g + 1)]),
                    wx=np.ascontiguousarray(w[:, 4096 + 512 * g:4096 + 512 * (g + 1)]),
                    wbc=np.ascontiguousarray(np.concatenate(
                        [w[:, 8192 + 128 * g:8192 + 128 * (g + 1)], w[:, 9216 + 128 * g:9216 + 128 * (g + 1)]], 1)),
                    wdt=np.ascontiguousarray(w[:, 10240 + 8 * g:10240 + 8 * (g + 1)]),
                    cw=cw, cb=cb, hp=np.ascontiguousarray(hp),
                    d_bc=_bc(np.repeat(f32(ssm_d[j])[8 * g:8 * g + 8], 64)),
                    ng_bc=_bc(f32(ssm_norm_g[j])[512 * g:512 * (g + 1)]),
                    ident=C["ident"], identf=C["identf"], trif=C["trif"], sel=sel))
            r = _run(build_ssm(S), maps)
            YT = np.concatenate([r[c]["yT"] for c in range(NCORES)], axis=0)
            KC = 32
            wp = f32(ssm_w_out[j])
        common = dict(wp=wp, g_bc=_bc(mlp_norm_g[i]), w1=f32(mlp_w_in[i]), w2=f32(mlp_w_out[i]), ident=C["ident"])
        maps = [dict(common, x=xs[c], yT=np.ascontiguousarray(YT[:, c * TPC:(c + 1) * TPC])) for c in range(NCORES)]
        r = _run(build_post_mlp(KC, TPC), maps)
        xs = [r[c]["x_out"] for c in range(NCORES)]
    return np.concatenate(xs, axis=0)[None].astype(np.float32)

QLR = 512
KVLR = 512
ROPE = 64
NOPE = 128
DV = 128
QK = 192
MLA_IN = QLR + KVLR + ROPE
ROPE_THETA = 10000.0


def build_mla_pre(TPC):
    nc = bass.Bass("TRN2", target_bir_lowering=False)
    NT = TPC // 512
    x_in = _dr(nc, "x", [TPC, D], F32, "ExternalInput")
    g_d = _dr(nc, "g_bc", [128, D], F32, "ExternalInput")
    id_d = _dr(nc, "ident", [128, 128], BF16, "ExternalInput")
    ones_d = _dr(nc, "ones", [128, 128], BF16, "ExternalInput")
    win_d = _dr(nc, "w_in", [D, MLA_IN], F32, "ExternalInput")
    gl_d = _dr(nc, "g_lat", [128, 8], F32, "ExternalInput")
    aT_d = _dr(nc, "aT", [MLA_IN, TPC], BF16, "ExternalOutput")
    P = Prog(nc)
    xt = P.sbuf("xt", [128, 4, D], F32)
    t_x = [P.T() for _ in range(4)]
    hT = P.sbuf("hT", [128, 16, 512], BF16)
    t_h = P.T()
    gbc = P.sbuf("gbc", [128, D], F32)
    t_g = P.T()
    ident = P.sbuf("ident_sb", [128, 128], BF16)
    t_id = P.T()
    ones = P.sbuf("ones_sb", [128, 128], BF16)
    t_ones = P.T()
    win = P.sbuf("win", [128, 16, MLA_IN], BF16)
    t_win = P.T()
    gl = P.sbuf("gl", [128, 8], F32)
    t_gl = P.T()
    sqb = P.sbuf("sqb", [128, 4, 512], BF16)
    t_sqb = [P.T() for _ in range(4)]
    rs = P.sbuf("rs", [128, 512], F32)
    t_rs = P.T()
    ao = [P.sbuf(f"ao{i}", [128, 4, 512], BF16) for i in range(2)]
    t_ao = [P.T(), P.T()]
    kr = P.sbuf("kr", [64, 512], BF16)
    t_kr = P.T()
    pg = [P.psum(f"pg{i}", [128, 512], F32) for i in range(4)]
    t_pg = [P.T() for _ in range(4)]
    pss = P.psum("pss", [128, 512], F32)
    t_pss = P.T()
    pr = P.psum("pr", [64, 512], F32)
    t_pr = P.T()
    P.dma(lambda e: e.dma_start(out=gbc[:], in_=g_d), writes=[t_g], key="c_g")
    P.dma(lambda e: e.dma_start(out=ident[:], in_=id_d), writes=[t_id], key="c_id")
    P.dma(lambda e: e.dma_start(out=ones[:], in_=ones_d), writes=[t_ones], key="c_ones")
    P.dma(lambda e: e.dma_start(out=gl[:], in_=gl_d), writes=[t_gl], key="c_gl")
    P.dma(lambda e: e.dma_start(out=win[:], in_=win_d.rearrange("(k p) n -> p k n", p=128)),
          writes=[t_win], key="c_win", eng="pool")
    nt = NormT(P, gbc, t_g, ident, t_id)
    for t in range(NT):
        tok0 = t * 512
        for blk in range(4):
            P.dma(lambda e, blk=blk, tok0=tok0: e.dma_start(
                out=xt[:, blk, :], in_=x_in[tok0 + blk * 128: tok0 + (blk + 1) * 128, :]),
                writes=[t_x[blk]], key=f"x{blk}")
        nt.emit(xt, t_x, hT, t_h)
        for grp in range(2):
            for j in range(4):
                c0 = (grp * 4 + j) * 128
                for k in range(16):
                    P.pe(lambda e, k=k, j=j, c0=c0: e.matmul(
                        pg[j][:], lhsT=win[:, k, c0:c0 + 128], rhs=hT[:, k, :],
                        start=(k == 0), stop=(k == 15)),
                        reads=[t_win, t_h], writes=[t_pg[j]])
                P.act(lambda e, j=j: e.activation(out=sqb[:, j, :], in_=pg[j][:], func=AF.Square),
                      reads=[t_pg[j]], writes=[t_sqb[j]])
            for j in range(4):
                P.pe(lambda e, j=j: e.matmul(pss[:], lhsT=ones[:], rhs=sqb[:, j, :],
                                             start=(j == 0), stop=(j == 3)),
                     reads=[t_ones, t_sqb[j]], writes=[t_pss])
            P.act(lambda e: e.activation(out=rs[:], in_=pss[:], func=AF.Sqrt, scale=1.0 / 512, bias=RMS_EPS),
                  reads=[t_pss], writes=[t_rs])
            P.dve(lambda e: e.reciprocal(out=rs[:], in_=rs[:]), reads=[t_rs], writes=[t_rs])
            a, t_a = ao[grp], t_ao[grp]
            for j in range(4):
                P.dve(lambda e, j=j, a=a, grp=grp: e.scalar_tensor_tensor(
                    out=a[:, j, :], in0=pg[j][:], scalar=gl[:, grp * 4 + j:grp * 4 + j + 1], in1=rs[:],
                    op0=ALU.mult, op1=ALU.mult),
                    reads=[t_pg[j], t_gl, t_rs], writes=[t_a])
            dst = aT_d[grp * 512:(grp + 1) * 512, tok0:tok0 + 512].rearrange("(j p) t -> p j t", p=128)
            P.dma(lambda e, dst=dst, a=a: e.dma_start(out=dst, in_=a[:]), reads=[t_a], key=f"ao{grp}")
        for k in range(16):
            P.pe(lambda e, k=k: e.matmul(pr[:], lhsT=win[:, k, 1024:1088], rhs=hT[:, k, :],
                                         start=(k == 0), stop=(k == 15)),
                 reads=[t_win, t_h], writes=[t_pr])
        P.act(lambda e: e.activation(out=kr[:], in_=pr[:], func=AF.Copy), reads=[t_pr], writes=[t_kr])
        P.dma(lambda e, tok0=tok0: e.dma_start(out=aT_d[1024:1088, tok0:tok0 + 512], in_=kr[:]),
              reads=[t_kr], key="kro")
    P.emit()
    return nc


def rope_consts():
    inv = (np.float32(ROPE_THETA) ** (-np.arange(0, ROPE, 2, dtype=np.float32) / np.float32(ROPE))).astype(np.float32)
    return np.concatenate([inv, inv]).reshape(64, 1).astype(np.float32)


PI_LO = 3.1415925
TWO_PI = 6.283185307179586
CW1 = 6.28125
CW2 = TWO_PI - CW1


def emit_rope_tables(P, S, pos_d, invf_d, cos_d, sin_d):
    CH = min(2048, S)
    posi = P.sbuf("rp_posi", [64, CH], DT.int32)
    t_posi = P.T()
    ang = P.sbuf("rp_ang", [64, CH], F32)
    t_ang = P.T()
    nn = P.sbuf("rp_n", [64, CH], F32)
    t_nn = P.T()
    ni = P.sbuf("rp_ni", [64, CH], DT.int32)
    t_ni = P.T()
    r = P.sbuf("rp_r", [64, CH], F32)
    t_r = P.T()
    w = P.sbuf("rp_w", [64, CH], F32)
    t_w = P.T()
    o = [P.sbuf(f"rp_o{i}", [64, CH], F32) for i in range(2)]
    t_o = [P.T(), P.T()]
    invf = P.sbuf("rp_invf", [64, 1], F32)
    t_invf = P.T()
    sgn = P.sbuf("rp_sgn", [64, 1], F32)
    t_sgn = P.T()
    P.dma(lambda e: e.dma_start(out=invf[:], in_=invf_d), writes=[t_invf], key="rp_c")
    P.dve(lambda e: e.memset(sgn[0:32, :], -1.0), writes=[t_sgn])
    P.dve(lambda e: e.memset(sgn[32:64, :], 1.0), writes=[t_sgn])
    t_cos, t_sin = P.T(), P.T()
    for c in range(S // CH):
        sl = slice(c * CH, (c + 1) * CH)
        P.dma(lambda e, sl=sl: e.dma_start(out=posi[:], in_=pos_d[:, sl]), writes=[t_posi], key="rp_pos")
        P.dve(lambda e: e.tensor_copy(out=ang[:], in_=posi[:]), reads=[t_posi], writes=[t_ang])
        P.dve(lambda e: e.tensor_scalar(out=ang[:], in0=ang[:], scalar1=invf[:, 0:1], scalar2=None,
                                        op0=ALU.mult), reads=[t_ang, t_invf], writes=[t_ang])
        P.dve(lambda e: e.tensor_scalar(out=ni[:], in0=ang[:], scalar1=1.0 / TWO_PI, scalar2=None,
                                        op0=ALU.mult), reads=[t_ang], writes=[t_ni])
        P.dve(lambda e: e.tensor_copy(out=nn[:], in_=ni[:]), reads=[t_ni], writes=[t_nn])
        P.dve(lambda e: e.scalar_tensor_tensor(out=r[:], in0=nn[:], scalar=-CW1, in1=ang[:],
                                               op0=ALU.mult, op1=ALU.add),
              reads=[t_nn, t_ang], writes=[t_r])
        P.dve(lambda e: e.scalar_tensor_tensor(out=r[:], in0=nn[:], scalar=-CW2, in1=r[:],
                                               op0=ALU.mult, op1=ALU.add),
              reads=[t_nn, t_r], writes=[t_r])
        for which in range(2):
            oo, t_oo = o[which], t_o[which]
            if which == 1:
                P.dve(lambda e: e.tensor_scalar(out=r[:], in0=r[:], scalar1=float(np.pi / 2), scalar2=None,
                                                op0=ALU.add), reads=[t_r], writes=[t_r])
            P.dve(lambda e: e.tensor_scalar(out=w[:], in0=r[:], scalar1=float(np.pi), scalar2=-TWO_PI,
                                            op0=ALU.is_gt, op1=ALU.mult), reads=[t_r], writes=[t_w])
            P.dve(lambda e: e.tensor_tensor(out=w[:], in0=w[:], in1=r[:], op=ALU.add),
                  reads=[t_w, t_r], writes=[t_w])
            P.dve(lambda e: e.tensor_scalar(out=w[:], in0=w[:], scalar1=PI_LO, scalar2=-PI_LO,
                                            op0=ALU.min, op1=ALU.max), reads=[t_w], writes=[t_w])
            if which == 0:
                P.act(lambda e, oo=oo: e.activation(out=oo[:], in_=w[:], func=AF.Sin), reads=[t_w], writes=[t_oo])
                P.dve(lambda e, oo=oo: e.tensor_scalar(out=oo[:], in0=oo[:], scalar1=sgn[:, 0:1], scalar2=None,
                                                       op0=ALU.mult), reads=[t_oo, t_sgn], writes=[t_oo])
                P.dma(lambda e, oo=oo, sl=sl: e.dma_start(out=sin_d[:, sl], in_=oo[:]), reads=[t_oo],
                      writes=[t_sin], key="rp_so")
            else:
                P.act(lambda e, oo=oo: e.activation(out=oo[:], in_=w[:], func=AF.Sin), reads=[t_w], writes=[t_oo])
                P.dma(lambda e, oo=oo, sl=sl: e.dma_start(out=cos_d[:, sl], in_=oo[:]), reads=[t_oo],
                      writes=[t_cos], key="rp_co")
    return t_cos, t_sin


def build_attn(S):
    nc = bass.Bass("TRN2", target_bir_lowering=False)
    NT = S // 512
    NB = S // 128
    AT = _dr(nc, "AT", [MLA_IN, S], BF16, "ExternalInput")
    pos_d = _dr(nc, "pos64", [64, S], DT.int32, "ExternalInput")
    invf_d = _dr(nc, "invf", [64, 1], F32, "ExternalInput")
    ones_d = _dr(nc, "ones", [128, 128], BF16, "ExternalInput")
    tri_d = _dr(nc, "tri", [128, 128], BF16, "ExternalInput")
    wq_d = _dr(nc, "wq", [2, QLR, 256], F32, "ExternalInput")
    wkv_d = _dr(nc, "wkv", [2, KVLR, 256], F32, "ExternalInput")
    gq_d = _dr(nc, "gq", [128, 3], F32, "ExternalInput")
    gk_d = _dr(nc, "gk", [128, 3], F32, "ExternalInput")
    oT_d = _dr(nc, "oT", [2 * DV, S], BF16, "ExternalOutput")
    dbg = bool(globals().get("ATTN_DEBUG"))
    cos_d = nc.dram_tensor("cos_scr", [64, S], F32, kind="ExternalOutput" if dbg else "Internal").ap()
    sin_d = nc.dram_tensor("sin_scr", [64, S], F32, kind="ExternalOutput" if dbg else "Internal").ap()
    P = Prog(nc)
    t_cos, t_sin = emit_rope_tables(P, S, pos_d, invf_d, cos_d, sin_d)
    scale = float(QK ** -0.5)

    ones = P.sbuf("ones_sb", [128, 128], BF16); t_ones = P.T()
    tri = P.sbuf("tri_sb", [128, 128], BF16); t_tri = P.T()
    wq = P.sbuf("wq_sb", [128, 4, 256], BF16); t_wq = P.T()
    wkv = P.sbuf("wkv_sb", [128, 4, 256], BF16); t_wkv = P.T()
    gq = P.sbuf("gq_sb", [128, 3], F32); t_gq = P.T()
    gk = P.sbuf("gk_sb", [128, 3], F32); t_gk = P.T()
    KTn = P.sbuf("KTn", [128, S], BF16)
    KTr = P.sbuf("KTr", [64, S], BF16)
    V = P.sbuf("V", [128, NB, 128], BF16)
    t_K = [P.T() for _ in range(NT)]
    lat = [P.sbuf(f"lat{i}", [128, 4, 512], BF16) for i in range(2)]; t_lat = [P.T(), P.T()]
    rr = [P.sbuf(f"rr{i}", [64, 2, 512], BF16) for i in range(2)]; t_rr = [P.T(), P.T()]
    cs = [P.sbuf(f"cs{i}", [64, 2, 512], F32) for i in range(2)]; t_cs = [P.T(), P.T()]
    sqn = P.sbuf("sqn", [128, 512], BF16); t_sqn = P.T()
    sqr = P.sbuf("sqr", [64, 512], BF16); t_sqr = P.T()
    rs = P.sbuf("rs", [128, 512], F32); t_rs = P.T()
    ra = P.sbuf("ra", [64, 512], F32); t_ra = P.T()
    rb = P.sbuf("rb", [64, 512], F32); t_rb = P.T()
    QTn = [P.sbuf(f"QTn{i}", [128, 512], BF16) for i in range(2)]; t_Qn = [P.T(), P.T()]
    QTr = [P.sbuf(f"QTr{i}", [64, 512], BF16) for i in range(2)]; t_Qr = [P.T(), P.T()]
    PT = [P.sbuf(f"PT{i}", [128, 512], BF16) for i in range(3)]; t_PT = [P.T() for _ in range(3)]
    rl = P.sbuf("rl", [128, 512], F32); t_rl = P.T()
    ob = [P.sbuf(f"ob{i}", [128, 512], BF16) for i in range(2)]; t_ob = [P.T(), P.T()]
    pS = [P.psum(f"pS{i}", [128, 512], F32) for i in range(2)]; t_pS = [P.T() for _ in range(2)]
    pO = P.psum("pO", [128, 512], F32); t_pO = P.T()
    pL = P.psum("pL", [128, 512], F32); t_pL = P.T()
    pm = [P.psum(f"pm{i}", [128, 512], F32) for i in range(4)]; t_pm = [P.T() for _ in range(4)]

    P.dma(lambda e: e.dma_start(out=ones[:], in_=ones_d), writes=[t_ones], key="c_ones")
    P.dma(lambda e: e.dma_start(out=tri[:], in_=tri_d), writes=[t_tri], key="c_tri")
    P.dma(lambda e: e.dma_start(out=gq[:], in_=gq_d), writes=[t_gq], key="c_gq")
    P.dma(lambda e: e.dma_start(out=gk[:], in_=gk_d), writes=[t_gk], key="c_gk")

    def norm_rope(pn, t_pn, pr_, t_pr, prs, t_prs, rope_from_sbuf, g, t_g_, csb, t_csb,
                  outn, t_outn, outr, t_outr):
        P.act(lambda e: e.activation(out=sqn[:], in_=pn, func=AF.Square), reads=[t_pn], writes=[t_sqn])
        P.act(lambda e: e.activation(out=sqr[:], in_=pr_, func=AF.Square), reads=[t_pr], writes=[t_sqr])
        pss, t_pss = pm[3], t_pm[3]
        P.pe(lambda e: e.matmul(pss[:], lhsT=ones[:], rhs=sqn[:], start=True, stop=False),
             reads=[t_ones, t_sqn], writes=[t_pss])
        P.pe(lambda e: e.matmul(pss[:], lhsT=ones[0:64, :], rhs=sqr[:], start=False, stop=True),
             reads=[t_ones, t_sqr], writes=[t_pss])
        P.act(lambda e: e.activation(out=rs[:], in_=pss[:], func=AF.Sqrt, scale=1.0 / QK, bias=RMS_EPS),
              reads=[t_pss], writes=[t_rs])
        P.dve(lambda e: e.reciprocal(out=rs[:], in_=rs[:]), reads=[t_rs], writes=[t_rs])
        P.dve(lambda e: e.scalar_tensor_tensor(out=outn, in0=pn, scalar=g[:, 0:1], in1=rs[:],
                                               op0=ALU.mult, op1=ALU.mult),
              reads=[t_pn, t_g_, t_rs], writes=[t_outn])
        P.dve(lambda e: e.scalar_tensor_tensor(out=ra[:], in0=pr_, scalar=g[0:64, 1:2], in1=rs[0:64, :],
                                               op0=ALU.mult, op1=ALU.mult),
              reads=[t_pr, t_g_, t_rs], writes=[t_ra])
        P.dve(lambda e: e.scalar_tensor_tensor(out=rb[:], in0=prs, scalar=g[0:64, 2:3], in1=rs[0:64, :],
                                               op0=ALU.mult, op1=ALU.mult),
              reads=[t_prs, t_g_, t_rs], writes=[t_rb])
        P.pool(lambda e: e.tensor_tensor(out=ra[:], in0=ra[:], in1=csb[:, 0, :], op=ALU.mult),
               reads=[t_ra, t_csb], writes=[t_ra])
        P.dve(lambda e: e.tensor_tensor(out=rb[:], in0=rb[:], in1=csb[:, 1, :], op=ALU.mult),
              reads=[t_rb, t_csb], writes=[t_rb])
        P.dve(lambda e: e.tensor_tensor(out=outr, in0=ra[:], in1=rb[:], op=ALU.add),
              reads=[t_ra, t_rb], writes=[t_outr])

    cnt = 0
    for h in range(2):
        P.dma(lambda e, h=h: e.dma_start(out=wq[:], in_=wq_d[h].rearrange("(k p) n -> p k n", p=128)),
              writes=[t_wq], key="c_wq", eng="pool")
        P.dma(lambda e, h=h: e.dma_start(out=wkv[:], in_=wkv_d[h].rearrange("(k p) n -> p k n", p=128)),
              writes=[t_wkv], key="c_wkv", eng="pool")
        for t in range(NT):
            sl = slice(t * 512, (t + 1) * 512)
            b = cnt % 2
            cnt += 1
            P.dma(lambda e, sl=sl, b=b: e.dma_start(
                out=lat[b][:], in_=AT[512:1024, sl].rearrange("(k p) t -> p k t", p=128)),
                writes=[t_lat[b]], key=f"lat{b}")
            P.dma(lambda e, sl=sl, b=b: [
                e.dma_start(out=rr[b][:, 0, :], in_=AT[1024:1088, sl]),
                e.dma_start(out=rr[b][0:32, 1, :], in_=AT[1056:1088, sl]),
                e.dma_start(out=rr[b][32:64, 1, :], in_=AT[1024:1056, sl])],
                writes=[t_rr[b]], key=f"rr{b}", ndma=3)
            P.dma(lambda e, sl=sl, b=b: [e.dma_start(out=cs[b][:, 0, :], in_=cos_d[:, sl]),
                                         e.dma_start(out=cs[b][:, 1, :], in_=sin_d[:, sl])],
                  reads=[t_cos, t_sin], writes=[t_cs[b]], key=f"cs{b}", ndma=2)
            pn, t_pn = pm[0], t_pm[0]
            for k in range(4):
                P.pe(lambda e, k=k, b=b, pn=pn: e.matmul(pn[:], lhsT=wkv[:, k, 0:128], rhs=lat[b][:, k, :],
                                                         start=(k == 0), stop=(k == 3)),
                     reads=[t_wkv, t_lat[b]], writes=[t_pn])
            norm_rope(pn[:], t_pn, rr[b][:, 0, :], t_rr[b], rr[b][:, 1, :], t_rr[b], True, gk, t_gk,
                      cs[b], t_cs[b], KTn[:, sl], t_K[t], KTr[:, sl], t_K[t])
            pv, t_pv = pm[1], t_pm[1]
            for blk in range(4):
                for k in range(4):
                    P.pe(lambda e, k=k, blk=blk, b=b, pv=pv: e.matmul(
                        pv[:, blk * 128:(blk + 1) * 128], lhsT=lat[b][:, k, blk * 128:(blk + 1) * 128],
                        rhs=wkv[:, k, 128:256], start=(k == 0), stop=(k == 3)),
                        reads=[t_wkv, t_lat[b]], writes=[t_pv])
            P.act(lambda e, t=t, pv=pv: e.activation(
                out=V[:, t * 4:(t + 1) * 4, :], in_=pv[:].rearrange("p (b c) -> p b c", b=4), func=AF.Copy),
                reads=[t_pv], writes=[t_K[t]])
        for qt in range(NT):
            sl = slice(qt * 512, (qt + 1) * 512)
            b = cnt % 2
            cnt += 1
            P.dma(lambda e, sl=sl, b=b: e.dma_start(
                out=lat[b][:], in_=AT[0:512, sl].rearrange("(k p) t -> p k t", p=128)),
                writes=[t_lat[b]], key=f"lat{b}")
            P.dma(lambda e, sl=sl, b=b: [e.dma_start(out=cs[b][:, 0, :], in_=cos_d[:, sl]),
                                         e.dma_start(out=cs[b][:, 1, :], in_=sin_d[:, sl])],
                  reads=[t_cos, t_sin], writes=[t_cs[b]], key=f"cs{b}", ndma=2)
            pn, t_pn = pm[0], t_pm[0]
            pq, t_pq = pm[1], t_pm[1]
            pqs, t_pqs = pm[2], t_pm[2]
            for k in range(4):
                P.pe(lambda e, k=k, b=b, pn=pn: e.matmul(pn[:], lhsT=wq[:, k, 0:128], rhs=lat[b][:, k, :],
                                                         start=(k == 0), stop=(k == 3)),
                     reads=[t_wq, t_lat[b]], writes=[t_pn])
            for k in range(4):
                P.pe(lambda e, k=k, b=b, pq=pq: e.matmul(pq[0:64, :], lhsT=wq[:, k, 128:192], rhs=lat[b][:, k, :],
                                                         start=(k == 0), stop=(k == 3)),
                     reads=[t_wq, t_lat[b]], writes=[t_pq])
            for k in range(4):
                P.pe(lambda e, k=k, b=b, pqs=pqs: e.matmul(pqs[0:64, :], lhsT=wq[:, k, 192:256], rhs=lat[b][:, k, :],
                                                           start=(k == 0), stop=(k == 3)),
                     reads=[t_wq, t_lat[b]], writes=[t_pqs])
            qn, t_qn, qr, t_qr = QTn[b], t_Qn[b], QTr[b], t_Qr[b]
            norm_rope(pn[:], t_pn, pq[0:64, :], t_pq, pqs[0:64, :], t_pqs, False, gq, t_gq,
                      cs[b], t_cs[b], qn[:], t_qn, qr[:], t_qr)
            nkb = 4 * qt + 4

            def s_mm(kb, qt=qt, qn=qn, qr=qr, t_qn=t_qn, t_qr=t_qr):
                c0 = max(0, kb - 4 * qt) * 128
                i = kb % 3
                s2 = kb % 2
                kt = kb // 4
                P.pe(lambda e: e.matmul(pS[s2][:, c0:512], lhsT=KTn[:, kb * 128:(kb + 1) * 128],
                                        rhs=qn[:, c0:512], start=True, stop=False),
                     reads=[t_K[kt], t_qn], writes=[t_pS[s2]])
                P.pe(lambda e: e.matmul(pS[s2][:, c0:512], lhsT=KTr[:, kb * 128:(kb + 1) * 128],
                                        rhs=qr[:, c0:512], start=False, stop=True),
                     reads=[t_K[kt], t_qr], writes=[t_pS[s2]])
                P.act(lambda e: e.activation(out=PT[i][:, c0:512], in_=pS[s2][:, c0:512], func=AF.Exp,
                                             scale=scale),
                      reads=[t_pS[s2]], writes=[t_PT[i]])
                if kb >= 4 * qt:
                    P.pool(lambda e: e.tensor_tensor(out=PT[i][:, c0:c0 + 128], in0=PT[i][:, c0:c0 + 128],
                                                     in1=tri[:], op=ALU.mult),
                           reads=[t_PT[i], t_tri], writes=[t_PT[i]])

            def pv_mm(kb, qt=qt):
                c0 = max(0, kb - 4 * qt) * 128
                i = kb % 3
                kt = kb // 4
                P.pe(lambda e: e.matmul(pO[:, c0:512], lhsT=V[:, kb, :], rhs=PT[i][:, c0:512],
                                        start=(kb == 0), stop=(kb == nkb - 1)),
                     reads=[t_K[kt], t_PT[i]], writes=[t_pO])
                P.pe(lambda e: e.matmul(pL[:, c0:512], lhsT=ones[:], rhs=PT[i][:, c0:512],
                                        start=(kb == 0), stop=(kb == nkb - 1)),
                     reads=[t_ones, t_PT[i]], writes=[t_pL])

            s_mm(0)
            if nkb > 1:
                s_mm(1)
            for kb in range(nkb):
                pv_mm(kb)
                if kb + 2 < nkb:
                    s_mm(kb + 2)
            P.dve(lambda e: e.reciprocal(out=rl[:], in_=pL[:]), reads=[t_pL], writes=[t_rl])
            o_, t_o_ = ob[b], t_ob[b]
            P.dve(lambda e, o_=o_: e.tensor_tensor(out=o_[:], in0=pO[:], in1=rl[:], op=ALU.mult),
                  reads=[t_pO, t_rl], writes=[t_o_])
            P.dma(lambda e, o_=o_, sl=sl, h=h: e.dma_start(out=oT_d[h * 128:(h + 1) * 128, sl], in_=o_[:]),
                  reads=[t_o_], key=f"oo{b}")
    P.emit()
    return nc

SSM_P = 64
SSM_N = 128
SSM_HG = 8
SSM_DG = 512
SSM_L = 256


def build_ssm(S):
    nc = bass.Bass("TRN2", target_bir_lowering=False)
    NT = S // 512
    HT = _dr(nc, "HT", [D, S], BF16, "ExternalInput")
    wz_d = _dr(nc, "wz", [D, 512], F32, "ExternalInput")
    wx_d = _dr(nc, "wx", [D, 512], F32, "ExternalInput")
    wbc_d = _dr(nc, "wbc", [D, 256], F32, "ExternalInput")
    wdt_d = _dr(nc, "wdt", [D, 8], F32, "ExternalInput")
    cw_d = _dr(nc, "cw", [128, 24], F32, "ExternalInput")
    cb_d = _dr(nc, "cb", [128, 6], F32, "ExternalInput")
    hp_d = _dr(nc, "hp", [8, 2], F32, "ExternalInput")
    dbc_d = _dr(nc, "d_bc", [128, 512], F32, "ExternalInput")
    ngbc_d = _dr(nc, "ng_bc", [128, 512], F32, "ExternalInput")
    idb_d = _dr(nc, "ident", [128, 128], BF16, "ExternalInput")
    idf_d = _dr(nc, "identf", [128, 128], F32, "ExternalInput")
    trif_d = _dr(nc, "trif", [128, 128], F32, "ExternalInput")
    sel_d = _dr(nc, "sel", [8, 1024], F32, "ExternalInput")
    yT_d = _dr(nc, "yT", [512, S], BF16, "ExternalOutput")
    P = Prog(nc)

    def SB(name, shape, dt_=F32):
        return P.sbuf(name, shape, dt_), P.T(name)

    wz, t_wz = SB("wz_sb", [128, 16, 512], BF16)
    wx, t_wx = SB("wx_sb", [128, 16, 512], BF16)
    wbc, t_wbc = SB("wbc_sb", [128, 16, 256], BF16)
    wdt, t_wdt = SB("wdt_sb", [128, 16, 8], BF16)
    cw, t_cw = SB("cw_sb", [128, 24])
    cb, t_cb = SB("cb_sb", [128, 6])
    hp, t_hp = SB("hp_sb", [8, 2])
    a8, t_a8 = SB("a8", [8, 1])
    dbc, t_dbc = SB("dbc_sb", [128, 512])
    ngbc, t_ngbc = SB("ngbc_sb", [128, 512])
    idb, t_idb = SB("idb", [128, 128], BF16)
    idf, t_idf = SB("idf", [128, 128])
    trif, t_trif = SB("trif_sb", [128, 128])
    sel, t_sel = SB("sel_sb", [8, 1024])
    ones8, t_ones8 = SB("ones8", [8, 256])
    hT = [P.sbuf(f"hT{i}", [128, 16, 512], BF16) for i in range(2)]
    t_hT = [P.T(), P.T()]
    raw = [P.sbuf(f"raw{i}", [128, 515], F32) for i in range(6)]
    t_raw = [P.T() for _ in range(6)]
    acc, t_acc = SB("acc", [128, 512])
    xsf = [P.sbuf(f"xsf{i}", [128, 512], F32) for i in range(4)]
    t_xsf = [P.T() for _ in range(4)]
    BTb, t_BTb = SB("BTb", [128, 512], BF16)
    CTf, t_CTf = SB("CTf", [128, 512])
    CTb, t_CTb = SB("CTb", [128, 512], BF16)
    zs, t_zs = SB("zs", [128, 4, 512]); t_zs = [P.T() for _ in range(4)]
    xst, _ = SB("xst", [128, 4, 512]); t_xst = [P.T() for _ in range(4)]
    xdt, _ = SB("xdt", [128, 4, 512], BF16); t_xdt = [P.T() for _ in range(4)]
    xdd, _ = SB("xdd", [128, 4, 512], BF16); t_xdd = [P.T() for _ in range(4)]
    Bt, t_Bt = SB("Bt", [128, 4, 128], BF16)
    dtr, t_dtr = SB("dtr", [8, 512])
    dta, t_dta = SB("dta", [8, 512])
    dte_, t_dte = SB("dte", [8, 512])
    cum, t_cum = SB("cum", [8, 512])
    dec, t_dec = SB("dec", [8, 512])
    tkn, t_tkn = SB("tkn", [128, 4, 24])
    cbm, t_cbm = SB("cbm", [128, 384])
    diff = [P.sbuf(f"diff{i}", [128, 384], F32) for i in range(2)]; t_diff = [P.T(), P.T()]
    Eb = [P.sbuf(f"Eb{i}", [128, 384], F32) for i in range(2)]; t_Eb = [P.T(), P.T()]
    MT = [P.sbuf(f"MT{i}", [128, 384], BF16) for i in range(2)]; t_MT = [P.T(), P.T()]
    Cs = [P.sbuf(f"Cs{i}", [128, 256], BF16) for i in range(2)]; t_Cs = [P.T(), P.T()]
    E0, _ = SB("E0", [128, 8, 256]); t_E0 = [P.T() for _ in range(8)]
    stT, t_stT = SB("stT", [128, 512])
    stb, t_stb = SB("stb", [128, 512], BF16)
    tmp, t_tmp = SB("tmp", [128, 512])
    y1, t_y1 = SB("y1", [128, 512])
    y3, t_y3 = SB("y3", [128, 512], BF16)
    junk, t_junk = SB("junk", [128, 512], BF16)
    ssq, _ = SB("ssq", [128, 8]); t_ssq = [P.T() for _ in range(4)]
    yTs = [P.sbuf(f"yTs{i}", [128, 4, 512], BF16) for i in range(2)]; t_yTs = [P.T(), P.T()]
    pin = [P.psum(f"pin{i}", [128, 512], F32) for i in range(2)]; t_pin = [P.T(), P.T()]
    ptf = P.psum("ptf", [128, 512], F32); t_ptf = P.T()
    ptb = P.psum("ptb", [128, 1024], BF16); t_ptb = P.T()
    pcb = P.psum("pcb", [128, 512], F32); t_pcb = P.T()
    pD = P.psum("pD", [128, 2, 256], F32); t_pD = [P.T(), P.T()]
    py = [P.psum(f"py{i}", [128, 512], F32) for i in range(2)]; t_py = [P.T(), P.T()]

    ld = lambda dst, src, t, key, eng="sp": P.dma(lambda e: e.dma_start(out=dst, in_=src), writes=[t], key=key, eng=eng)
    ld(wz[:], wz_d.rearrange("(k p) n -> p k n", p=128), t_wz, "c_wz", "pool")
    ld(wx[:], wx_d.rearrange("(k p) n -> p k n", p=128), t_wx, "c_wx", "pool")
    ld(wbc[:], wbc_d.rearrange("(k p) n -> p k n", p=128), t_wbc, "c_wbc", "pool")
    ld(wdt[:], wdt_d.rearrange("(k p) n -> p k n", p=128), t_wdt, "c_wdt", "pool")
    ld(cw[:], cw_d, t_cw, "c_cw")
    ld(cb[:], cb_d, t_cb, "c_cb")
    ld(hp[:], hp_d, t_hp, "c_hp")
    ld(dbc[:], dbc_d, t_dbc, "c_dbc")
    ld(ngbc[:], ngbc_d, t_ngbc, "c_ngbc")
    ld(idb[:], idb_d, t_idb, "c_idb")
    ld(idf[:], idf_d, t_idf, "c_idf")
    ld(trif[:], trif_d, t_trif, "c_trif")
    ld(sel[:], sel_d, t_sel, "c_sel")
    P.dve(lambda e: e.memset(ones8[:], 1.0), writes=[t_ones8])
    P.dve(lambda e: e.memset(stT[:], 0.0), writes=[t_stT])
    P.dve(lambda e: e.memset(stb[:], 0.0), writes=[t_stb])
    for i in range(6):
        P.pool(lambda e, i=i: e.memset(raw[i][:, 0:3], 0.0), writes=[t_raw[i]])
    P.act(lambda e: e.activation(out=a8[:], in_=hp[:, 1:2], func=AF.Exp), reads=[t_hp], writes=[t_a8])
    P.dve(lambda e: e.tensor_scalar(out=a8[:], in0=a8[:], scalar1=-1.0, scalar2=None, op0=ALU.mult),
          reads=[t_a8], writes=[t_a8])

    pin_i = 0
    hcnt = 0
    for tt in range(NT):
        sl = slice(tt * 512, (tt + 1) * 512)
        hb = tt % 2
        h_, t_h = hT[hb], t_hT[hb]
        P.dma(lambda e, sl=sl, h_=h_: e.dma_start(out=h_[:], in_=HT[:, sl].rearrange("(k p) t -> p k t", p=128)),
              writes=[t_h], key=f"hT{hb}")
        for blk in range(4):
            pi = pin_i % 2; pin_i += 1
            for k in range(16):
                P.pe(lambda e, k=k, blk=blk, pi=pi, h_=h_: e.matmul(
                    pin[pi][:], lhsT=h_[:, k, blk * 128:(blk + 1) * 128], rhs=wz[:, k, :],
                    start=(k == 0), stop=(k == 15)), reads=[t_h, t_wz], writes=[t_pin[pi]])
            P.act(lambda e, blk=blk, pi=pi: e.activation(out=zs[:, blk, :], in_=pin[pi][:], func=AF.Silu),
                  reads=[t_pin[pi]], writes=[t_zs[blk]])
        for c in range(6):
            pi = pin_i % 2; pin_i += 1
            for k in range(16):
                lw = wx[:, k, c * 128:(c + 1) * 128] if c < 4 else wbc[:, k, (c - 4) * 128:(c - 3) * 128]
                P.pe(lambda e, k=k, pi=pi, lw=lw, h_=h_: e.matmul(
                    pin[pi][:], lhsT=lw, rhs=h_[:, k, :], start=(k == 0), stop=(k == 15)),
                    reads=[t_h, t_wx, t_wbc], writes=[t_pin[pi]])
            P.act(lambda e, c=c, pi=pi: e.activation(out=raw[c][:, 3:515], in_=pin[pi][:], func=AF.Copy),
                  reads=[t_pin[pi]], writes=[t_raw[c]])
        pi = pin_i % 2; pin_i += 1
        for k in range(16):
            P.pe(lambda e, k=k, pi=pi, h_=h_: e.matmul(pin[pi][0:8, :], lhsT=wdt[:, k, 0:8], rhs=h_[:, k, :],
                                                      start=(k == 0), stop=(k == 15)),
                 reads=[t_h, t_wdt], writes=[t_pin[pi]])
        P.act(lambda e, pi=pi: e.activation(out=dtr[:], in_=pin[pi][0:8, :], func=AF.Identity, bias=hp[:, 0:1]),
              reads=[t_pin[pi], t_hp], writes=[t_dtr])
        P.act(lambda e: e.activation(out=dta[:], in_=dtr[:], func=AF.Abs), reads=[t_dtr], writes=[t_dta])
        P.act(lambda e: e.activation(out=dta[:], in_=dta[:], func=AF.Exp, scale=-1.0), reads=[t_dta], writes=[t_dta])
        P.act(lambda e: e.activation(out=dta[:], in_=dta[:], func=AF.Ln, bias=1.0), reads=[t_dta], writes=[t_dta])
        P.dve(lambda e: e.scalar_tensor_tensor(out=dte_[:], in0=dtr[:], scalar=0.0, in1=dta[:],
                                               op0=ALU.max, op1=ALU.add),
              reads=[t_dtr, t_dta], writes=[t_dte])
        P.dve(lambda e: e.tensor_scalar(out=dtr[:], in0=dte_[:], scalar1=a8[:, 0:1], scalar2=None, op0=ALU.mult),
              reads=[t_dte, t_a8], writes=[t_dtr])
        for ci in range(2):
            cs_ = slice(ci * 256, (ci + 1) * 256)
            P.dve(lambda e, cs_=cs_: e.tensor_tensor_scan(out=cum[:, cs_], data0=ones8[:], data1=dtr[:, cs_],
                                                          initial=0.0, op0=ALU.mult, op1=ALU.add),
                  reads=[t_dtr, t_ones8], writes=[t_cum])
            P.act(lambda e, cs_=cs_, ci=ci: e.activation(out=dec[:, cs_], in_=cum[:, cs_], func=AF.Exp, scale=-1.0,
                                                         bias=cum[:, ci * 256 + 255: ci * 256 + 256]),
                  reads=[t_cum], writes=[t_dec])
        for blk in range(4):
            for j, (src, t_src) in enumerate(((dte_, t_dte), (cum, t_cum), (dec, t_dec))):
                P.pe(lambda e, blk=blk, j=j, src=src: e.transpose(
                    ptf[:, blk * 24 + j * 8: blk * 24 + j * 8 + 8], src[0:8, blk * 128:(blk + 1) * 128], idf[0:8, 0:8]),
                    reads=[t_src, t_idf], writes=[t_ptf])
        P.dve(lambda e: e.tensor_copy(out=tkn[:].rearrange("p b j -> p (b j)"), in_=ptf[:, 0:96]),
              reads=[t_ptf], writes=[t_tkn])
        for c in range(6):
            P.dve(lambda e, c=c: e.tensor_scalar(out=acc[:], in0=raw[c][:, 0:512], scalar1=cw[:, c * 4:c * 4 + 1],
                                                 scalar2=None, op0=ALU.mult),
                  reads=[t_raw[c], t_cw], writes=[t_acc])
            for k in range(1, 4):
                P.dve(lambda e, c=c, k=k: e.scalar_tensor_tensor(
                    out=acc[:], in0=raw[c][:, k:k + 512], scalar=cw[:, c * 4 + k:c * 4 + k + 1], in1=acc[:],
                    op0=ALU.mult, op1=ALU.add), reads=[t_raw[c], t_cw, t_acc], writes=[t_acc])
            if c < 4:
                P.act(lambda e, c=c: e.activation(out=xsf[c][:], in_=acc[:], func=AF.Silu, bias=cb[:, c:c + 1]),
                      reads=[t_acc, t_cb], writes=[t_xsf[c]])
            elif c == 4:
                P.act(lambda e, c=c: e.activation(out=BTb[:], in_=acc[:], func=AF.Silu, bias=cb[:, c:c + 1]),
                      reads=[t_acc, t_cb], writes=[t_BTb])
            else:
                P.act(lambda e, c=c: e.activation(out=CTf[:], in_=acc[:], func=AF.Silu, bias=cb[:, c:c + 1]),
                      reads=[t_acc, t_cb], writes=[t_CTf])
                P.pool(lambda e: e.tensor_copy(out=CTb[:], in_=CTf[:]), reads=[t_CTf], writes=[t_CTb])
            P.pool(lambda e, c=c: e.tensor_copy(out=raw[c][:, 0:3], in_=raw[c][:, 512:515]),
                   reads=[t_raw[c]], writes=[t_raw[c]])
        for blk in range(4):
            for c in range(4):
                P.pe(lambda e, blk=blk, c=c: e.transpose(ptf[:, c * 128:(c + 1) * 128],
                                                        xsf[c][:, blk * 128:(blk + 1) * 128], idf[:]),
                     reads=[t_xsf[c], t_idf], writes=[t_ptf])
            P.act(lambda e, blk=blk: e.activation(out=xst[:, blk, :], in_=ptf[:], func=AF.Copy),
                  reads=[t_ptf], writes=[t_xst[blk]])
        for blk in range(4):
            P.pe(lambda e, blk=blk: e.transpose(ptb[:, blk * 128:(blk + 1) * 128],
                                                BTb[:, blk * 128:(blk + 1) * 128], idb[:]),
                 reads=[t_BTb, t_idb], writes=[t_ptb])
        P.dve(lambda e: e.tensor_copy(out=Bt[:].rearrange("p b n -> p (b n)"), in_=ptb[:, 0:512]),
              reads=[t_ptb], writes=[t_Bt])
        for blk in range(4):
            for h in range(8):
                hs = slice(h * 64, (h + 1) * 64)
                P.dve(lambda e, blk=blk, h=h, hs=hs: e.tensor_scalar(
                    out=xdt[:, blk, hs], in0=xst[:, blk, hs], scalar1=tkn[:, blk, h:h + 1], scalar2=None,
                    op0=ALU.mult), reads=[t_xst[blk], t_tkn], writes=[t_xdt[blk]])
                P.pool(lambda e, blk=blk, h=h, hs=hs: e.tensor_scalar(
                    out=xdd[:, blk, hs], in0=xst[:, blk, hs], scalar1=tkn[:, blk, h:h + 1],
                    scalar2=tkn[:, blk, 16 + h:17 + h], op0=ALU.mult, op1=ALU.mult),
                    reads=[t_xst[blk], t_tkn], writes=[t_xdd[blk]])
        yb = tt % 2
        for ci in range(2):
            c0 = ci * 256
            b0, b1 = ci * 2, ci * 2 + 1
            P.pe(lambda e, c0=c0: e.matmul(pcb[:, 0:256], lhsT=BTb[:, c0:c0 + 128], rhs=CTb[:, c0:c0 + 256],
                                           start=True, stop=True), reads=[t_BTb, t_CTb], writes=[t_pcb])
            P.pe(lambda e, c0=c0: e.matmul(pcb[:, 256:384], lhsT=BTb[:, c0 + 128:c0 + 256],
                                           rhs=CTb[:, c0 + 128:c0 + 256], start=True, stop=True),
                 reads=[t_BTb, t_CTb], writes=[t_pcb])
            P.dve(lambda e: e.tensor_tensor(out=cbm[:, 0:128], in0=pcb[:, 0:128], in1=trif[:], op=ALU.mult),
                  reads=[t_pcb, t_trif], writes=[t_cbm])
            P.act(lambda e: e.activation(out=cbm[:, 128:256], in_=pcb[:, 128:256], func=AF.Copy),
                  reads=[t_pcb], writes=[t_cbm])
            P.dve(lambda e: e.tensor_tensor(out=cbm[:, 256:384], in0=pcb[:, 256:384], in1=trif[:], op=ALU.mult),
                  reads=[t_pcb, t_trif], writes=[t_cbm])
            for h in range(8):
                i = hcnt % 2; hcnt += 1
                hs = slice(h * 64, (h + 1) * 64)
                P.pe(lambda e, h=h, i=i, c0=c0: e.matmul(pD[:, i, :], lhsT=sel[0:8, h * 128:(h + 1) * 128],
                                                         rhs=cum[0:8, c0:c0 + 256], start=True, stop=True),
                     reads=[t_sel, t_cum], writes=[t_pD[i]])
                P.dve(lambda e, h=h, i=i, b0=b0: e.tensor_scalar(
                    out=diff[i][:, 0:256], in0=pD[:, i, :], scalar1=tkn[:, b0, 8 + h:9 + h], scalar2=0.0,
                    op0=ALU.subtract, op1=ALU.min), reads=[t_pD[i], t_tkn], writes=[t_diff[i]])
                P.dve(lambda e, h=h, i=i, b1=b1: e.tensor_scalar(
                    out=diff[i][:, 256:384], in0=pD[:, i, 128:256], scalar1=tkn[:, b1, 8 + h:9 + h], scalar2=0.0,
                    op0=ALU.subtract, op1=ALU.min), reads=[t_pD[i], t_tkn], writes=[t_diff[i]])
                P.act(lambda e, i=i: e.activation(out=Eb[i][:], in_=diff[i][:], func=AF.Exp),
                      reads=[t_diff[i]], writes=[t_Eb[i]])
                P.act(lambda e, i=i, h=h: e.activation(out=E0[:, h, :], in_=pD[:, i, :], func=AF.Exp),
                      reads=[t_pD[i]], writes=[t_E0[h]])
                P.pool(lambda e, i=i: e.tensor_tensor(out=MT[i][:], in0=Eb[i][:], in1=cbm[:], op=ALU.mult),
                       reads=[t_Eb[i], t_cbm], writes=[t_MT[i]])
                P.pool(lambda e, i=i, h=h, c0=c0: e.tensor_tensor(out=Cs[i][:], in0=CTf[:, c0:c0 + 256],
                                                                  in1=E0[:, h, :], op=ALU.mult),
                       reads=[t_CTf, t_E0[h]], writes=[t_Cs[i]])
                P.pe(lambda e, i=i, hs=hs, b0=b0: e.matmul(py[0][:, hs], lhsT=MT[i][:, 0:128], rhs=xdt[:, b0, hs],
                                                           start=True, stop=False),
                     reads=[t_MT[i], t_xdt[b0]], writes=[t_py[0]])
                P.pe(lambda e, i=i, hs=hs: e.matmul(py[0][:, hs], lhsT=Cs[i][:, 0:128], rhs=stb[:, hs],
                                                    start=False, stop=True),
                     reads=[t_Cs[i], t_stb], writes=[t_py[0]])
                P.pe(lambda e, i=i, hs=hs, b0=b0: e.matmul(py[1][:, hs], lhsT=MT[i][:, 128:256], rhs=xdt[:, b0, hs],
                                                           start=True, stop=False),
                     reads=[t_MT[i], t_xdt[b0]], writes=[t_py[1]])
                P.pe(lambda e, i=i, hs=hs, b1=b1: e.matmul(py[1][:, hs], lhsT=MT[i][:, 256:384], rhs=xdt[:, b1, hs],
                                                           start=False, stop=False),
                     reads=[t_MT[i], t_xdt[b1]], writes=[t_py[1]])
                P.pe(lambda e, i=i, hs=hs: e.matmul(py[1][:, hs], lhsT=Cs[i][:, 128:256], rhs=stb[:, hs],
                                                    start=False, stop=True),
                     reads=[t_Cs[i], t_stb], writes=[t_py[1]])
            pst, t_pst = pin[pin_i % 2], t_pin[pin_i % 2]
            pin_i += 1
            P.pe(lambda e, b0=b0, pst=pst: e.matmul(pst[:], lhsT=Bt[:, b0, :], rhs=xdd[:, b0, :], start=True, stop=False),
                 reads=[t_Bt, t_xdd[b0]], writes=[t_pst])
            P.pe(lambda e, b1=b1, pst=pst: e.matmul(pst[:], lhsT=Bt[:, b1, :], rhs=xdd[:, b1, :], start=False, stop=True),
                 reads=[t_Bt, t_xdd[b1]], writes=[t_pst])
            for h in range(8):
                hs = slice(h * 64, (h + 1) * 64)
                P.dve(lambda e, h=h, hs=hs, pst=pst: e.scalar_tensor_tensor(
                    out=stT[:, hs], in0=stT[:, hs], scalar=E0[:, h, 255:256], in1=pst[:, hs],
                    op0=ALU.mult, op1=ALU.add), reads=[t_stT, t_E0[h], t_pst], writes=[t_stT])
            P.act(lambda e: e.activation(out=stb[:], in_=stT[:], func=AF.Copy), reads=[t_stT], writes=[t_stb])
            for lb in range(2):
                blk = ci * 2 + lb
                P.pool(lambda e, blk=blk: e.tensor_tensor(out=tmp[:], in0=xst[:, blk, :], in1=dbc[:], op=ALU.mult),
                       reads=[t_xst[blk], t_dbc], writes=[t_tmp])
                P.dve(lambda e, lb=lb: e.tensor_tensor(out=y1[:], in0=py[lb][:], in1=tmp[:], op=ALU.add),
                      reads=[t_py[lb], t_tmp], writes=[t_y1])
                P.dve(lambda e, blk=blk: e.tensor_tensor(out=y1[:], in0=y1[:], in1=zs[:, blk, :], op=ALU.mult),
                      reads=[t_y1, t_zs[blk]], writes=[t_y1])
                P.act(lambda e, blk=blk: e.activation(out=junk[:], in_=y1[:], func=AF.Square,
                                                      accum_out=ssq[:, blk:blk + 1]),
                      reads=[t_y1], writes=[t_junk, t_ssq[blk]])
                P.act(lambda e, blk=blk: e.activation(out=ssq[:, 4 + blk:5 + blk], in_=ssq[:, blk:blk + 1],
                                                      func=AF.Sqrt, scale=1.0 / SSM_DG, bias=RMS_EPS),
                      reads=[t_ssq[blk]], writes=[t_ssq[blk]])
                P.dve(lambda e, blk=blk: e.reciprocal(out=ssq[:, 4 + blk:5 + blk], in_=ssq[:, 4 + blk:5 + blk]),
                      reads=[t_ssq[blk]], writes=[t_ssq[blk]])
                P.dve(lambda e, blk=blk: e.scalar_tensor_tensor(
                    out=y3[:], in0=y1[:], scalar=ssq[:, 4 + blk:5 + blk], in1=ngbc[:], op0=ALU.mult, op1=ALU.mult),
                    reads=[t_y1, t_ssq[blk], t_ngbc], writes=[t_y3])
                for c in range(4):
                    P.pe(lambda e, c=c: e.transpose(ptb[:, c * 128:(c + 1) * 128], y3[:, c * 128:(c + 1) * 128], idb[:]),
                         reads=[t_y3, t_idb], writes=[t_ptb])
                P.act(lambda e, blk=blk, yb=yb: e.activation(
                    out=yTs[yb][:, :, blk * 128:(blk + 1) * 128],
                    in_=ptb[:, 0:512].rearrange("p (c t) -> p c t", c=4), func=AF.Copy),
                    reads=[t_ptb], writes=[t_yTs[yb]])
        P.dma(lambda e, sl=sl, yb=yb: e.dma_start(out=yT_d[:, sl].rearrange("(c p) t -> p c t", p=128), in_=yTs[yb][:]),
              reads=[t_yTs[yb]], key=f"yo{yb}")
    P.emit()
    return nc
```
